# Optimizing a Trainium2 kernel written in Bass

```python
import math
import jax, jax.numpy as jnp
from jax import lax
import numpy as np


D_MODEL = 2048
BATCH = 4
SEQ = 4096
DEPTH = 2

CTX_LEN = 256
GRID_W = 64
EPS = 1e-6
N_BRANCH = 3

D_A = D_MODEL // 2
CONV_A_W = 31

N_DIFF_HEADS = 8
DIFF_QK_DIM = 64
DIFF_V_DIM = 2 * DIFF_QK_DIM
D_B = N_DIFF_HEADS * DIFF_V_DIM
ROPE_BASE = 10000.0
Q_BLOCK = 128

N_GLA_HEADS = 4
D_C = D_MODEL // 2
DK_C = D_C // 2
GLA_HK = DK_C // N_GLA_HEADS
GLA_HV = D_C // N_GLA_HEADS
GATE_RANK = 16
GATE_TAU = 16.0
GLA_CHUNK = 64

D_FF = 11 * D_MODEL // 4
CONV_F_W = 3

IN_SPLITS = (2 * D_A, 2 * N_DIFF_HEADS * DIFF_QK_DIM, 2 * N_DIFF_HEADS * DIFF_QK_DIM, D_B,
             DK_C, DK_C, D_C, D_C, 2 * GATE_RANK, N_BRANCH * D_MODEL)
D_IN = sum(IN_SPLITS)

kernel_name = 'hybrid_diffusion_trunk'


def rms_norm(x, g):
    xf = x.astype(jnp.float32)
    y = xf * lax.rsqrt(jnp.mean(xf * xf, axis=-1, keepdims=True) + EPS)
    return (y * g.astype(jnp.float32)).astype(x.dtype)


def layer_norm(x, g, b):
    xf = x.astype(jnp.float32)
    mu = jnp.mean(xf, axis=-1, keepdims=True)
    xc = xf - mu
    var = jnp.mean(xc * xc, axis=-1, keepdims=True)
    return (xc * lax.rsqrt(var + EPS) * g.astype(jnp.float32) + b.astype(jnp.float32)).astype(x.dtype)


def modulate(h, shift, scale):
    return h * (1 + scale) + shift


def dw_conv(x, w, b):
    k = w.shape[0]
    pad = (k - 1) // 2
    y = lax.conv_general_dilated(x, w[:, None, :].astype(x.dtype), window_strides=(1,),
                                 padding=[(pad, pad)], dimension_numbers=('NWC', 'WIO', 'NWC'),
                                 feature_group_count=x.shape[-1])
    return y + b


def split_proj(u):
    return jnp.split(u, np.cumsum(IN_SPLITS)[:-1].tolist(), axis=-1)


def axial_rope(n_tokens):
    rows = n_tokens // GRID_W
    row = jnp.broadcast_to(jnp.arange(rows, dtype=jnp.float32)[:, None], (rows, GRID_W)).reshape(-1)
    col = jnp.broadcast_to(jnp.arange(GRID_W, dtype=jnp.float32)[None, :], (rows, GRID_W)).reshape(-1)
    n_freq = DIFF_QK_DIM // 4
    inv = ROPE_BASE ** (-jnp.arange(n_freq, dtype=jnp.float32) / n_freq)
    ang = jnp.concatenate([row[:, None] * inv, col[:, None] * inv], axis=-1)
    return jnp.cos(ang), jnp.sin(ang)


def apply_rope(x, cos, sin):
    half = x.shape[-1] // 2
    x1, x2 = x[..., :half], x[..., half:]
    cos = cos[:, None, :].astype(x.dtype)
    sin = sin[:, None, :].astype(x.dtype)
    return jnp.concatenate([x1 * cos - x2 * sin, x1 * sin + x2 * cos], axis=-1)


def conformer_branch(u, conv_w, conv_b, ln_g, ln_b, w_out):
    val, gate = jnp.split(u, 2, axis=-1)
    h = val * jax.nn.sigmoid(gate)
    h = dw_conv(h, conv_w, conv_b)
    h = jax.nn.silu(layer_norm(h, ln_g, ln_b))
    return h @ w_out


def diff_qkv(q, k, v, qn_g, kn_g, rope):
    b_, n_ = q.shape[:2]
    q = rms_norm(q.reshape(b_, n_, 2 * N_DIFF_HEADS, DIFF_QK_DIM), qn_g)
    k = rms_norm(k.reshape(b_, n_, 2 * N_DIFF_HEADS, DIFF_QK_DIM), kn_g)
    if rope is not None:
        q = apply_rope(q, rope[0], rope[1])
        k = apply_rope(k, rope[0], rope[1])
    q = q * DIFF_QK_DIM ** -0.5
    v = v.reshape(b_, n_, N_DIFF_HEADS, DIFF_V_DIM)
    return q.transpose(0, 2, 1, 3), k.transpose(0, 2, 1, 3), v.transpose(0, 2, 1, 3)


def diff_attend(q, k, v, lam, lam_init, subln_g):
    s = jnp.einsum('bhqd,bhkd->bhqk', q, k).astype(jnp.float32)
    p = jax.nn.softmax(s, axis=-1)
    b_, h2, nq, nk = p.shape
    p = p.reshape(b_, h2 // 2, 2, nq, nk)
    w = p[:, :, 0] - lam * p[:, :, 1]
    o = jnp.einsum('bhqk,bhkv->bhqv', w.astype(v.dtype), v)
    return rms_norm(o, subln_g) * (1.0 - lam_init)


def diff_attn_latent(q, k_all, v_all, lam, lam_init, subln_g):
    b_, h2, n_, d = q.shape
    nb = n_ // Q_BLOCK
    qb = q.reshape(b_, h2, nb, Q_BLOCK, d).transpose(2, 0, 1, 3, 4)
    ob = lax.map(lambda qq: diff_attend(qq, k_all, v_all, lam, lam_init, subln_g), qb)
    return ob.transpose(1, 0, 3, 2, 4).reshape(b_, n_, D_B)


def gla_heads(t, hd):
    b_, n_ = t.shape[:2]
    return t.reshape(b_, n_, N_GLA_HEADS, hd).transpose(0, 2, 1, 3)


def gla_log_gates(lr, w2, b):
    b_, n_ = lr.shape[:2]
    z = jnp.einsum('bnzr,zrk->bnzk', lr.reshape(b_, n_, 2, GATE_RANK), w2) + b
    la = jax.nn.log_sigmoid(z.astype(jnp.float32)) / GATE_TAU
    return gla_heads(la[:, :, 0], GLA_HK), gla_heads(la[:, :, 1], GLA_HK)


def gla_scan(q, k, v, log_a, s0):
    b_, h_, n_, dk = q.shape
    dv = v.shape[-1]
    nc = n_ // GLA_CHUNK
    mask = jnp.tril(jnp.ones((GLA_CHUNK, GLA_CHUNK), dtype=bool))

    def to_chunks(t):
        return t.reshape(b_, h_, nc, GLA_CHUNK, t.shape[-1]).transpose(2, 0, 1, 3, 4)

    def step(s, inp):
        qc, kc, vc, gc = (t.astype(jnp.float32) for t in inp)
        bcum = jnp.cumsum(gc, axis=2)
        diff = bcum[:, :, :, None, :] - bcum[:, :, None, :, :]
        decay = jnp.exp(jnp.where(mask[:, :, None], diff, -jnp.inf))
        a = jnp.einsum('bhid,bhjd,bhijd->bhij', qc, kc, decay)
        o = jnp.einsum('bhij,bhjv->bhiv', a, vc) + jnp.einsum('bhid,bhdv->bhiv', qc * jnp.exp(bcum), s)
        b_last = bcum[:, :, -1:, :]
        s_new = jnp.exp(b_last[:, :, 0, :])[..., None] * s + jnp.einsum('bhjd,bhjv->bhdv', kc * jnp.exp(b_last - bcum), vc)
        return s_new, o

    s_fin, o = lax.scan(step, s0, (to_chunks(q), to_chunks(k), to_chunks(v), to_chunks(log_a)))
    o = o.transpose(1, 2, 0, 3, 4).reshape(b_, h_, n_, dv)
    return o.astype(v.dtype), s_fin


def gla_bidir(q, k, v, la_f, la_b, s0_f, s0_b):
    o_f, s_f = gla_scan(q, k, v, la_f, s0_f)
    flip = lambda t: jnp.flip(t, axis=2)
    o_b, s_b = gla_scan(flip(q), flip(k), flip(v), flip(la_b), s0_b)
    return o_f + flip(o_b), s_f, s_b


def gla_inputs(qg, kg, vg, lr, w2, b):
    la_f, la_b = gla_log_gates(lr, w2, b)
    return gla_heads(qg * GLA_HK ** -0.5, GLA_HK), gla_heads(kg, GLA_HK), gla_heads(vg, GLA_HV), la_f, la_b


def gla_out(o, r, gn_g, w_out):
    b_, h_, n_, dv = o.shape
    o = rms_norm(o, gn_g).transpose(0, 2, 1, 3).reshape(b_, n_, h_ * dv)
    return (o * jax.nn.silu(r)) @ w_out


def merge_branches(ya, yb, yc, gate_pre, b_gate, w_o):
    b_, n_ = gate_pre.shape[:2]
    g = jax.nn.sigmoid(gate_pre.reshape(b_, n_, N_BRANCH, D_MODEL) + b_gate)
    return (g[:, :, 0] * ya + g[:, :, 1] * yb + g[:, :, 2] * yc) @ w_o


def conv_ffn(h, w_up, conv_w, conv_b, w_down):
    u = dw_conv(h @ w_up, conv_w, conv_b)
    val, gate = jnp.split(u, 2, axis=-1)
    return (jax.nn.silu(gate) * val) @ w_down


def setup_inputs(seed: int = 0) -> dict:
    key = jax.random.key(seed)
    ks = iter(jax.random.split(key, 40))
    L, D = DEPTH, D_MODEL

    def nrm(shape, scale):
        return jax.random.normal(next(ks), shape, jnp.float32) * scale

    return {
        'x': nrm((BATCH, SEQ, D), 1.0),
        'c': nrm((BATCH, D), 1.0),
        'ctx': nrm((BATCH, CTX_LEN, D), 1.0),
        'c_ctx': nrm((D,), 1.0),
        'w_ada': nrm((L, D, 6 * D), 0.5 * D ** -0.5),
        'b_ada': nrm((L, 6 * D), 0.02),
        'g_norm1': 1.0 + nrm((L, D), 0.02),
        'w_in': nrm((L, D, D_IN), D ** -0.5),
        'b_gate': nrm((L, N_BRANCH, D), 0.02),
        'conv_a_w': nrm((L, CONV_A_W, D_A), CONV_A_W ** -0.5),
        'conv_a_b': nrm((L, D_A), 0.02),
        'ln_a_g': 1.0 + nrm((L, D_A), 0.02),
        'ln_a_b': nrm((L, D_A), 0.02),
        'w_a_out': nrm((L, D_A, D), D_A ** -0.5),
        'qn_g': 1.0 + nrm((L, DIFF_QK_DIM), 0.02),
        'kn_g': 1.0 + nrm((L, DIFF_QK_DIM), 0.02),
        'lam_q1': nrm((L, DIFF_QK_DIM), 0.1),
        'lam_k1': nrm((L, DIFF_QK_DIM), 0.1),
        'lam_q2': nrm((L, DIFF_QK_DIM), 0.1),
        'lam_k2': nrm((L, DIFF_QK_DIM), 0.1),
        'subln_g': 1.0 + nrm((L, DIFF_V_DIM), 0.02),
        'w_b_out': nrm((L, D_B, D), D_B ** -0.5),
        'w_alpha2': nrm((L, 2, GATE_RANK, DK_C), GATE_RANK ** -0.5),
        'b_alpha': nrm((L, 2, DK_C), 0.02),
        'gn_c_g': 1.0 + nrm((L, GLA_HV), 0.02),
        'w_c_out': nrm((L, D_C, D), D_C ** -0.5),
        'w_o': nrm((L, D, D), D ** -0.5),
        'g_norm2': 1.0 + nrm((L, D), 0.02),
        'w_up': nrm((L, D, 2 * D_FF), D ** -0.5),
        'conv_f_w': nrm((L, CONV_F_W, 2 * D_FF), CONV_F_W ** -0.5),
        'conv_f_b': nrm((L, 2 * D_FF), 0.02),
        'w_down': nrm((L, D_FF, D), D_FF ** -0.5),
    }


def reference(x, c, ctx, c_ctx, w_ada, b_ada, g_norm1, w_in, b_gate, conv_a_w, conv_a_b, ln_a_g, ln_a_b,
              w_a_out, qn_g, kn_g, lam_q1, lam_k1, lam_q2, lam_k2, subln_g, w_b_out, w_alpha2, b_alpha,
              gn_c_g, w_c_out, w_o, g_norm2, w_up, conv_f_w, conv_f_b, w_down):
    b_ = x.shape[0]
    n_lat = x.shape[1]
    rope = axial_rope(n_lat)
    s_zero = jnp.zeros((b_, N_GLA_HEADS, GLA_HK, GLA_HV), jnp.float32)

    for l in range(DEPTH):
        last = l == DEPTH - 1
        lam_init = 0.8 - 0.6 * math.exp(-0.3 * l)

        mod_l = (jax.nn.silu(c) @ w_ada[l] + b_ada[l])[:, None, :]
        mod_c = jax.nn.silu(c_ctx) @ w_ada[l] + b_ada[l]
        sh1, sc1, gt1, sh2, sc2, gt2 = jnp.split(mod_l, 6, axis=-1)
        csh1, csc1, cgt1, csh2, csc2, cgt2 = jnp.split(mod_c, 6, axis=-1)

        h_l = modulate(rms_norm(x, g_norm1[l]), sh1, sc1)
        h_c = modulate(rms_norm(ctx, g_norm1[l]), csh1, csc1)
        a_l, qd_l, kd_l, vd_l, qg_l, kg_l, vg_l, r_l, lr_l, gate_l = split_proj(h_l @ w_in[l])
        a_c, qd_c, kd_c, vd_c, qg_c, kg_c, vg_c, r_c, lr_c, gate_c = split_proj(h_c @ w_in[l])

        lam = (jnp.exp(jnp.sum(lam_q1[l] * lam_k1[l]).astype(jnp.float32))
               - jnp.exp(jnp.sum(lam_q2[l] * lam_k2[l]).astype(jnp.float32)) + lam_init)
        q_l, k_l, v_l = diff_qkv(qd_l, kd_l, vd_l, qn_g[l], kn_g[l], rope)
        q_c, k_c, v_c = diff_qkv(qd_c, kd_c, vd_c, qn_g[l], kn_g[l], None)
        k_all = jnp.concatenate([k_l, k_c], axis=2)
        v_all = jnp.concatenate([v_l, v_c], axis=2)
        yb_l = diff_attn_latent(q_l, k_all, v_all, lam, lam_init, subln_g[l]) @ w_b_out[l]

        gq_c, gk_c, gv_c, laf_c, lab_c = gla_inputs(qg_c, kg_c, vg_c, lr_c, w_alpha2[l], b_alpha[l])
        gq_l, gk_l, gv_l, laf_l, lab_l = gla_inputs(qg_l, kg_l, vg_l, lr_l, w_alpha2[l], b_alpha[l])
        o_c, s_f, s_b = gla_bidir(gq_c, gk_c, gv_c, laf_c, lab_c, s_zero, s_zero)
        o_l, _, _ = gla_bidir(gq_l, gk_l, gv_l, laf_l, lab_l, s_f, s_b)
        yc_l = gla_out(o_l, r_l, gn_c_g[l], w_c_out[l])

        ya_l = conformer_branch(a_l, conv_a_w[l], conv_a_b[l], ln_a_g[l], ln_a_b[l], w_a_out[l])

        x_mid = x + gt1 * merge_branches(ya_l, yb_l, yc_l, gate_l, b_gate[l], w_o[l])

        h2 = modulate(rms_norm(x_mid, g_norm2[l]), sh2, sc2)
        x_new = x_mid + gt2 * conv_ffn(h2, w_up[l], conv_f_w[l], conv_f_b[l], w_down[l])

        if not last:
            yb_c = diff_attend(q_c, k_c, v_c, lam, lam_init, subln_g[l])
            yb_c = yb_c.transpose(0, 2, 1, 3).reshape(b_, ctx.shape[1], D_B) @ w_b_out[l]
            yc_c = gla_out(o_c, r_c, gn_c_g[l], w_c_out[l])
            ya_c = conformer_branch(a_c, conv_a_w[l], conv_a_b[l], ln_a_g[l], ln_a_b[l], w_a_out[l])
            ctx_mid = ctx + cgt1 * merge_branches(ya_c, yb_c, yc_c, gate_c, b_gate[l], w_o[l])
            h2c = modulate(rms_norm(ctx_mid, g_norm2[l]), csh2, csc2)
            ctx = ctx_mid + cgt2 * conv_ffn(h2c, w_up[l], conv_f_w[l], conv_f_b[l], w_down[l])

        x = x_new

    return x
```

```python
import math
from contextlib import ExitStack

import numpy as np
import concourse.bass as bass
import concourse.mybir as mybir
from concourse.bass_utils import run_bass_kernel_spmd

F32 = mybir.dt.float32
BF16 = mybir.dt.bfloat16
AF = mybir.ActivationFunctionType
ALU = mybir.AluOpType

D = 2048
KC = 16
EPS = 1e-6
DA = 1024
DFF = 5632
NCH_IN = 112
SEG = dict(a=(0, 16), qd=(16, 24), kd=(24, 32), vd=(32, 40), qg=(40, 44), kg=(44, 48),
           vg=(48, 56), r=(56, 64), gate=(64, 112))

_pv = {}
_o = 0
for _n, _w in [('g1', 16), ('g2', 16), ('b_ada', 96), ('b_gate', 48), ('caw', 8 * 31), ('cab', 8),
               ('lag', 8), ('lab', 8), ('qn', 1), ('kn', 1), ('subln', 1), ('gn', 2),
               ('cfw', 88 * 3), ('cfb', 88), ('bb', 1024), ('lam', 256)]:
    _pv[_n] = (_o, _w)
    _o += _w
NPV = _o
_cv = {}
_o = 0
for _n in ['ident', 'ones', 'bones64', 'rotT', 'tri_f', 'tri_b', 'mask_f', 'mask_b']:
    _cv[_n] = _o
    _o += 128
NCONST = _o


class _Eng:
    def __init__(self, name, obj, sem):
        self.name, self.obj, self.sem = name, obj, sem
        self.count = 0
        self.waited = {}


class _Slot:
    def __init__(self, sem):
        self.sem = sem
        self.count = 0


class Res:
    _all = []

    def __init__(self, name=''):
        self.name = name
        self.w = None
        self.r = {}
        Res._all.append(self)


class Sched:
    NSLOT = 6

    def __init__(self, nc, es):
        self.nc = nc
        self.engs = {}
        for name, obj in [('pe', nc.tensor), ('act', nc.scalar), ('dve', nc.vector),
                          ('pool', nc.gpsimd), ('sp', nc.sync)]:
            sem = es.enter_context(nc.semaphore('s_' + name))
            self.engs[name] = _Eng(name, obj, sem)
        self.slots = {}
        self.rr = {}
        for q in ('sp', 'act', 'pool'):
            self.slots[q] = [_Slot(es.enter_context(nc.semaphore('d_%s_%d' % (q, i))))
                             for i in range(self.NSLOT)]
            self.rr[q] = 0
        self.n_wait = 0
        self.n_ins = 0
        self.arrive = es.enter_context(nc.semaphore('s_arrive'))
        self.go = es.enter_context(nc.semaphore('s_go'))
        self.epoch = 0

    def reset(self):
        self.barrier()
        for e in self.engs.values():
            e.obj.sem_inc(self.arrive, 1)
        self.epoch += 1
        sp = self.engs['sp']
        sp.obj.wait_ge(self.arrive, 5 * self.epoch)
        for e in self.engs.values():
            if e.count:
                sp.obj.sem_clear(e.sem)
        for q in self.slots:
            for sl in self.slots[q]:
                if sl.count:
                    sp.obj.sem_clear(sl.sem)
        sp.obj.sem_inc(self.go, 1)
        for e in self.engs.values():
            if e.name != 'sp':
                e.obj.wait_ge(self.go, self.epoch)
        for e in self.engs.values():
            e.count = 0
            e.waited = {}
        for q in self.slots:
            for sl in self.slots[q]:
                sl.count = 0
        for r in Res._all:
            r.w = None
            r.r = {}

    def _need(self, eng, deps):
        best = {}
        for sem, val in deps:
            if best.get(sem.name, (None, 0))[1] < val:
                best[sem.name] = (sem, val)
        for nm, (sem, val) in best.items():
            if eng.waited.get(nm, 0) < val:
                eng.obj.wait_ge(sem, val)
                eng.waited[nm] = val
                self.n_wait += 1

    def _deps(self, eng, reads, writes):
        deps = []
        for r in reads:
            if r.w is not None:
                deps.append(r.w)
        for w in writes:
            if w.w is not None:
                deps.append(w.w)
            deps.extend(w.r.values())
        if eng.name == 'pe':
            deps = [d for d in deps if d[0].name != eng.sem.name]
        return deps

    def _commit(self, dep, reads, writes):
        nm = dep[0].name
        for r in reads:
            r.r[nm] = dep
        for w in writes:
            w.w = dep
            w.r = {}

    def op(self, en, fn, reads=(), writes=()):
        eng = self.engs[en]
        self._need(eng, self._deps(eng, reads, writes))
        ins = fn(eng.obj)
        eng.count += 1
        ins.then_inc(eng.sem, 1)
        self.n_ins += 1
        self._commit((eng.sem, eng.count), reads, writes)

    def dma(self, q, out, in_, reads=(), writes=()):
        eng = self.engs[q]
        sl = self.slots[q][self.rr[q]]
        self.rr[q] = (self.rr[q] + 1) % self.NSLOT
        deps = self._deps(eng, reads, writes)
        if sl.count:
            deps.append((sl.sem, 16 * sl.count))
        self._need(eng, deps)
        ins = eng.obj.dma_start(out=out, in_=in_)
        sl.count += 1
        ins.then_inc(sl.sem, 16)
        self.n_ins += 1
        self._commit((sl.sem, 16 * sl.count), reads, writes)

    def barrier(self):
        deps = [(e.sem, e.count) for e in self.engs.values() if e.count]
        for q in self.slots:
            deps += [(s.sem, 16 * s.count) for s in self.slots[q] if s.count]
        for e in self.engs.values():
            self._need(e, deps)


class Ring:
    def __init__(self, items):
        self.items = items
        self.i = 0

    def next(self):
        it = self.items[self.i]
        self.i = (self.i + 1) % len(self.items)
        return it


def _ffn_tiles(n):
    nt = -(-n // 510)
    w = -(-n // nt)
    return [(a, min(a + w, n)) for a in range(0, n, w)]


def build(NLAT, NCTX, DEPTH=2, phases=None, dbg=False, nolast=False):
    T = NLAT + NCTX
    Res._all = []
    nc = bass.Bass("TRN2", target_bir_lowering=False)
    kind_s = "ExternalOutput" if dbg else "Internal"

    def din(name, shape, dt=F32):
        return nc.dram_tensor(name, list(shape), dt, kind="ExternalInput").ap()

    def dscr(name, shape, dt=F32):
        return nc.dram_tensor(name, list(shape), dt, kind=kind_s).ap()

    x0 = din("x0", [D, T])
    cc_in = din("cc", [128, KC, 2])
    consts_in = din("consts", [128, NCONST])
    cos_in = din("cosT", [128, T])
    sin_in = din("sinT", [128, T])
    pv_in = din("pv", [DEPTH, 128, NPV])
    w_ada = din("w_ada", [DEPTH, 96, 128, KC, 128])
    w_in = din("w_in", [DEPTH, NCH_IN, 128, KC, 128])
    w_lr = din("w_lr", [DEPTH, 128, KC, 32])
    w2p = din("w2p", [DEPTH, 32, 2, 512])
    w_abc = din("w_abc", [DEPTH, 3, 16, 128, 8, 128])
    w_o = din("w_o", [DEPTH, 16, 128, KC, 128])
    w_up = din("w_up", [DEPTH, 88, 128, KC, 128])
    w_down = din("w_down", [DEPTH, 16, 128, 44, 128])
    yT = nc.dram_tensor("yT", [D, NLAT], F32, kind="ExternalOutput").ap()

    xa = dscr("xa", [D, T])
    xb = dscr("xb", [D, T])
    gluT = dscr("gluT", [DA, T])
    convT = dscr("convT", [DA, T])
    qT = dscr("qT", [1024, T], BF16)
    kT = dscr("kT", [1024, T], BF16)
    vtm = dscr("vtm", [T, 1024], BF16)
    gqT = dscr("gqT", [512, T])
    gkT = dscr("gkT", [512, T])
    gvtm = dscr("gvtm", [T, 1024], BF16)
    rsT = dscr("rsT", [1024, T])
    lrT = dscr("lrT", [32, T])
    gatesT = dscr("gatesT", [3 * D, T])
    attnT = dscr("attnT", [1024, T], BF16)
    ofT = dscr("ofT", [1024, T])
    glaT = dscr("glaT", [1024, T], BF16)

    with ExitStack() as es:
        S = Sched(nc, es)

        _uid = [0]

        def sb(name, shape, dt=F32, stack=es):
            _uid[0] += 1
            return stack.enter_context(nc.sbuf_tensor("%s_%d" % (name, _uid[0]), list(shape), dt))

        consts = sb("consts_sb", [128, NCONST]); r_consts = Res()
        pvt = [sb("pv%d" % l, [128, NPV]) for l in range(DEPTH)]; r_pv = [Res() for _ in range(DEPTH)]
        mods = [sb("mods%d" % l, [128, 96, 2]) for l in range(DEPTH)]; r_mods = [Res() for _ in range(DEPTH)]
        m1p = [sb("m1p%d" % l, [128, 32, 2]) for l in range(DEPTH)]
        lamt = [sb("lamt%d" % l, [128, 4]) for l in range(DEPTH)]; r_lam = [Res() for _ in range(DEPTH)]
        cbf = sb("cbf", [128, 3 * 128], BF16); r_cbf = Res()
        epst = sb("epst", [128, 2]); r_eps = Res()
        psum = [es.enter_context(nc.psum_tensor("ps%d" % i, [128, 512], F32)) for i in range(7)]
        psb = es.enter_context(nc.psum_tensor("psb", [128, 1024], BF16))
        r_ps = [Res("ps%d" % i) for i in range(7)]
        r_psb = Res("psb")

        def C(name):
            o = _cv[name]
            return consts[:, o:o + 128]

        def PV(l, name, j=None, n=1):
            o, w = _pv[name]
            if j is None:
                return pvt[l][:, o:o + w]
            return pvt[l][:, o + j:o + j + n]

        S.dma('sp', consts[:], consts_in[:, :], writes=[r_consts])
        for l in range(DEPTH):
            S.dma('sp', pvt[l][:], pv_in[l, :, :], writes=[r_pv[l]])
        S.op('dve', lambda e: e.tensor_copy(cbf[:, 0:128], C('ident')), [r_consts], [r_cbf])
        S.op('dve', lambda e: e.tensor_copy(cbf[:, 128:256], C('ones')), [r_consts], [r_cbf])
        S.op('dve', lambda e: e.memset(epst[:, 0:1], EPS), [], [r_eps])
        S.op('dve', lambda e: e.memset(epst[:, 1:2], 1.0), [], [r_eps])
        ident_bf = cbf[:, 0:128]
        ones_bf = cbf[:, 128:256]
        eps_ap = epst[:, 0:1]
        one_ap = epst[:, 1:2]

        with ExitStack() as ps_:
            cct = sb("cct", [128, KC, 2], stack=ps_); r_cc = Res()
            wad = [sb("wad%d" % i, [128, 4, KC, 128], stack=ps_) for i in range(2)]
            r_wad = [Res() for _ in range(2)]
            tmp = sb("adatmp", [128, 64], stack=ps_); r_tmp = Res()
            S.dma('sp', cct[:], cc_in[:, :, :], writes=[r_cc])
            sg = sb("ccsg", [128, KC, 2], stack=ps_); r_sg = Res()
            S.op('act', lambda e: e.activation(out=sg[:], in_=cct[:], func=AF.Sigmoid), [r_cc], [r_sg])
            S.op('dve', lambda e: e.tensor_tensor(out=cct[:], in0=cct[:], in1=sg[:], op=ALU.mult), [r_sg, r_cc], [r_cc])
            for l in range(DEPTH):
                pacc = psum[l]
                for g in range(24):
                    wb = wad[g % 2]; rw = r_wad[g % 2]
                    S.dma('sp', wb[:], w_ada[l, 4 * g:4 * g + 4].rearrange("c p k n -> p c k n"), writes=[rw])
                    for ci in range(4):
                        c = 4 * g + ci
                        for k in range(KC):
                            S.op('pe', lambda e, wb=wb, ci=ci, k=k, c=c, pacc=pacc: e.matmul(
                                pacc[:, 2 * c:2 * c + 2], wb[:, ci, k, :], cct[:, k, :],
                                start=(k == 0), stop=(k == KC - 1)), [rw, r_cc], [r_ps[l]])
                for j in range(2):
                    S.op('dve', lambda e, l=l, j=j, pacc=pacc: e.tensor_tensor(
                        out=mods[l][:, :, j], in0=pacc[:, 0:192].rearrange("p (c j) -> p c j", j=2)[:, :, j],
                        in1=PV(l, 'b_ada'), op=ALU.add), [r_ps[l], r_pv[l]], [r_mods[l]])
                for j in range(2):
                    for (dst, scn, gn) in ((0, 16, 'g1'), (16, 64, 'g2')):
                        S.op('dve', lambda e, l=l, j=j, dst=dst, scn=scn, gn=gn: e.scalar_tensor_tensor(
                            out=m1p[l][:, dst:dst + 16, j], in0=mods[l][:, scn:scn + 16, j], scalar=1.0,
                            in1=PV(l, gn), op0=ALU.add, op1=ALU.mult), [r_mods[l], r_pv[l]], [r_mods[l]])
                lam_init = 0.8 - 0.6 * math.exp(-0.3 * l)
                lo = _pv['lam'][0]
                S.op('dve', lambda e, l=l: e.tensor_tensor(out=tmp[:, 0:64], in0=pvt[l][:, lo:lo + 64],
                                                         in1=pvt[l][:, lo + 64:lo + 128], op=ALU.mult), [r_pv[l]], [r_tmp])
                S.op('dve', lambda e, l=l: e.reduce_sum(out=lamt[l][:, 2:3], in_=tmp[:, 0:64], axis=mybir.AxisListType.X),
                     [r_tmp], [r_lam[l]])
                S.op('dve', lambda e, l=l: e.tensor_tensor(out=tmp[:, 0:64], in0=pvt[l][:, lo + 128:lo + 192],
                                                         in1=pvt[l][:, lo + 192:lo + 256], op=ALU.mult), [r_pv[l], r_lam[l]], [r_tmp])
                S.op('dve', lambda e, l=l: e.reduce_sum(out=lamt[l][:, 3:4], in_=tmp[:, 0:64], axis=mybir.AxisListType.X),
                     [r_tmp], [r_lam[l]])
                S.op('act', lambda e, l=l: e.activation(out=lamt[l][:, 2:4], in_=lamt[l][:, 2:4], func=AF.Exp),
                     [r_lam[l]], [r_lam[l]])
                S.op('dve', lambda e, l=l: e.scalar_tensor_tensor(
                    out=lamt[l][:, 0:1], in0=lamt[l][:, 2:3], scalar=lam_init, in1=lamt[l][:, 3:4],
                    op0=ALU.add, op1=ALU.subtract), [r_lam[l]], [r_lam[l]])
                S.op('dve', lambda e, l=l: e.tensor_scalar(
                    out=lamt[l][:, 1:2], in0=PV(l, 'subln'), scalar1=1.0 - lam_init, scalar2=None, op0=ALU.mult),
                    [r_pv[l], r_lam[l]], [r_lam[l]])
            S.barrier()

        _stg = {}

        def wload(dst, src, rdst, nel):
            st, rst = _stg['ring'].next()
            S.dma('sp', st[:, :nel], src, writes=[rst])
            S.op('pool', lambda e: e.tensor_copy(out=dst, in_=st[:, :nel]), [rst], [rdst])

        def stg_alloc(stack):
            _stg['ring'] = Ring([(sb("wstg%d" % i, [128, 4096], stack=stack), Res()) for i in range(2)])

        def MOD(l, name, k, j):
            base = dict(sh1=0, sc1=16, gt1=32, sh2=48, sc2=64, gt2=80)[name]
            return mods[l][:, base + k, j:j + 1]

        def tok_tiles(last):
            tl = [(t, 512, 0) for t in range(0, NLAT, 512)]
            tl.append((NLAT, NCTX, 1))
            return tl

        def rms_rstd(ss_ps, r_ss, n, out_t, r_out, W):
            S.op('act', lambda e: e.activation(out=out_t[:, :W], in_=ss_ps[:, :W], func=AF.Sqrt,
                                               bias=eps_ap, scale=1.0 / n), [r_ss, r_eps], [r_out])
            S.op('dve', lambda e: e.reciprocal(out=out_t[:, :W], in_=out_t[:, :W]), [r_out], [r_out])

        for l in range(DEPTH):
            last = (l == DEPTH - 1) and not nolast
            xin = x0 if l == 0 else xb
            xout = xb if not last else None
            do = (lambda p: phases is None or p in phases)

            if do(1):
              with ExitStack() as ps_:
                xt = [sb("p1x0", [128, KC, 512], stack=ps_)] * 2; r_xt = [Res()] * 2
                sq = [sb("p1sq%d" % i, [128, 512], stack=ps_) for i in range(2)]; r_sq = [Res() for _ in range(2)]
                ht = [sb("p1h0", [128, KC, 512], BF16, stack=ps_)] * 2; r_ht = [Res()] * 2
                rstd = sb("p1rstd", [128, 512], stack=ps_); r_rstd = Res()
                wr = Ring([(sb("p1w%d" % i, [128, 4, KC, 128], BF16, stack=ps_), Res()) for i in range(3)])
                stg_alloc(ps_)
                wlr = sb("p1wlr", [128, KC, 32], BF16, stack=ps_); r_wlr = Res()
                ost = Ring([(sb("p1o%d" % i, [128, 512], stack=ps_), Res()) for i in range(4)])
                osb = Ring([(sb("p1ob%d" % i, [128, 512], BF16, stack=ps_), Res()) for i in range(3)])
                tmpA = Ring([(sb("p1ta%d" % i, [128, 512], stack=ps_), Res()) for i in range(3)])
                xg = Ring([(sb("p1xg%d" % i, [128, 512], stack=ps_), Res()) for i in range(2)])
                rs2 = sb("p1rs2", [128, 512], stack=ps_); r_rs2 = Res()
                cs = [sb("p1cos", [128, 512], stack=ps_), sb("p1sin", [128, 512], stack=ps_)]; r_cs = Res()
                pr = Ring([(psum[i], r_ps[i]) for i in range(4)])
                wload(wlr[:].rearrange("p k n -> p (k n)"), w_lr[l, :, :, :].rearrange("p k n -> p (k n)"), r_wlr, KC * 32)
                tiles = tok_tiles(last)
                for ti, (t0, W, j) in enumerate(tiles):
                    b = ti % 2
                    S.dma('sp', xt[b][:, :, :W], xin.rearrange("(k p) t -> p k t", p=128)[:, :, t0:t0 + W], writes=[r_xt[b]])
                    S.dma('sp', cs[0][:, :W], cos_in[:, t0:t0 + W], writes=[r_cs])
                    S.dma('sp', cs[1][:, :W], sin_in[:, t0:t0 + W], writes=[r_cs])
                    ssp, r_ssp = psum[4], r_ps[4]
                    for k in range(KC):
                        sqb, r_sqb = sq[k % 2], r_sq[k % 2]
                        S.op('act', lambda e, sqb=sqb, k=k: e.activation(out=sqb[:, :W], in_=xt[b][:, k, :W], func=AF.Square),
                             [r_xt[b]], [r_sqb])
                        S.op('pe', lambda e, sqb=sqb, k=k: e.matmul(ssp[:, :W], C('ones'), sqb[:, :W], start=(k == 0), stop=(k == KC - 1)),
                             [r_sqb, r_consts], [r_ssp])
                    rms_rstd(ssp, r_ssp, D, rstd, r_rstd, W)
                    for k in range(KC):
                        sqb, r_sqb = sq[k % 2], r_sq[k % 2]
                        S.op('dve', lambda e, sqb=sqb, k=k: e.tensor_tensor(out=sqb[:, :W], in0=xt[b][:, k, :W], in1=rstd[:, :W], op=ALU.mult),
                             [r_xt[b], r_rstd], [r_sqb])
                        S.op('act', lambda e, sqb=sqb, k=k: e.activation(out=ht[b][:, k, :W], in_=sqb[:, :W], func=AF.Identity,
                                                                       bias=MOD(l, 'sh1', k, j), scale=m1p[l][:, k, j:j + 1]),
                             [r_sqb, r_mods[l]], [r_ht[b]])
                    h = ht[b]; r_h = r_ht[b]
                    ctx_skip = last and j == 1
                    for g in range(NCH_IN // 4):
                        segn = [n for n, (a, bb_) in SEG.items() if a <= 4 * g < bb_][0]
                        if ctx_skip and segn not in ('kd', 'vd', 'kg', 'vg'):
                            continue
                        wb, rw = wr.next()
                        for hh in range(2):
                            wload(wb[:, 2 * hh:2 * hh + 2].rearrange("p c k n -> p c (k n)"),
                                  w_in[l, 4 * g + 2 * hh:4 * g + 2 * hh + 2].rearrange("c p k n -> p c (k n)"), rw, 4096)
                        if segn in ('vd', 'vg'):
                            dst = vtm if segn == 'vd' else gvtm
                            c0 = (4 * g - SEG[segn][0]) * 128
                            for s in range(W // 128):
                                pp, rp = pr.next()
                                for k in range(KC):
                                    S.op('pe', lambda e, pp=pp, k=k, s=s, wb=wb: e.matmul(
                                        pp[:, :].rearrange("p (c n) -> p c n", c=4), h[:, k, s * 128:(s + 1) * 128], wb[:, :, k, :],
                                        start=(k == 0), stop=(k == KC - 1)), [r_h, rw], [rp])
                                ob, rob = osb.next()
                                S.op('act', lambda e, ob=ob, pp=pp: e.copy(out=ob[:, :], in_=pp[:, :]), [rp], [rob])
                                S.dma('act', dst[t0 + s * 128:t0 + (s + 1) * 128, c0:c0 + 512], ob[:, :], reads=[rob])
                            continue
                        pcs = []
                        for ci in range(4):
                            c = 4 * g + ci
                            pp, rp = pr.next()
                            for k in range(KC):
                                S.op('pe', lambda e, pp=pp, k=k, ci=ci, wb=wb: e.matmul(
                                    pp[:, :W], wb[:, ci, k, :], h[:, k, :W], start=(k == 0), stop=(k == KC - 1)), [r_h, rw], [rp])
                            pcs.append((c, pp, rp))
                            if segn == 'a':
                                if ci % 2 == 1:
                                    (_, pv_, rpv), (_, pg_, rpg) = pcs[-2], pcs[-1]
                                    ta, rta = tmpA.next()
                                    S.op('act', lambda e, ta=ta, pg_=pg_: e.activation(out=ta[:, :W], in_=pg_[:, :W], func=AF.Sigmoid), [rpg], [rta])
                                    ot, rot = ost.next()
                                    S.op('dve', lambda e, ot=ot, ta=ta, pv_=pv_: e.tensor_tensor(out=ot[:, :W], in0=pv_[:, :W], in1=ta[:, :W], op=ALU.mult),
                                         [rpv, rta], [rot])
                                    cc_ = c // 2
                                    S.dma('act', gluT[cc_ * 128:(cc_ + 1) * 128, t0:t0 + W], ot[:, :W], reads=[rot])
                            elif segn in ('qd', 'kd'):
                                isq = segn == 'qd'
                                gn = PV(l, 'qn' if isq else 'kn')
                                xgt, rxg = xg.next()
                                sqt, rsq = tmpA.next()
                                S.op('act', lambda e, xgt=xgt, pp=pp, gn=gn: e.activation(out=xgt[:, :W], in_=pp[:, :W], func=AF.Copy, scale=gn),
                                     [rp, r_pv[l]], [rxg])
                                S.op('act', lambda e, sqt=sqt, pp=pp: e.activation(out=sqt[:, :W], in_=pp[:, :W], func=AF.Square), [rp], [rsq])
                                p2, rp2 = psum[5], r_ps[5]
                                p3, rp3 = psum[6], r_ps[6]
                                S.op('pe', lambda e, sqt=sqt: e.matmul(p2[:, :W], C('bones64'), sqt[:, :W], start=True, stop=True), [rsq, r_consts], [rp2])
                                S.op('pe', lambda e, xgt=xgt: e.matmul(p3[:, :W], C('rotT'), xgt[:, :W], start=True, stop=True), [rxg, r_consts], [rp3])
                                rms_rstd(p2, rp2, 64, rs2, r_rs2, W)
                                t1, rt1 = tmpA.next()
                                S.op('dve', lambda e, t1=t1, xgt=xgt: e.tensor_tensor(out=t1[:, :W], in0=xgt[:, :W], in1=cs[0][:, :W], op=ALU.mult), [rxg, r_cs], [rt1])
                                S.op('dve', lambda e, xgt=xgt: e.tensor_tensor(out=xgt[:, :W], in0=p3[:, :W], in1=cs[1][:, :W], op=ALU.mult), [rp3, r_cs], [rxg])
                                S.op('pool', lambda e, t1=t1, xgt=xgt: e.tensor_tensor(out=t1[:, :W], in0=t1[:, :W], in1=xgt[:, :W], op=ALU.add), [rt1, rxg], [rt1])
                                ob, rob = osb.next()
                                S.op('dve', lambda e, ob=ob, t1=t1, isq=isq: e.scalar_tensor_tensor(
                                    out=ob[:, :W], in0=t1[:, :W], scalar=(0.125 if isq else 1.0), in1=rs2[:, :W], op0=ALU.mult, op1=ALU.mult),
                                    [rt1, r_rs2], [rob])
                                dst = qT if isq else kT
                                cc_ = c - SEG[segn][0]
                                S.dma('act', dst[cc_ * 128:(cc_ + 1) * 128, t0:t0 + W], ob[:, :W], reads=[rob])
                            elif segn in ('qg', 'kg'):
                                ot, rot = ost.next()
                                sc_ = (128.0 ** -0.5) if segn == 'qg' else 1.0
                                S.op('act', lambda e, ot=ot, pp=pp, sc_=sc_: e.activation(out=ot[:, :W], in_=pp[:, :W], func=AF.Copy, scale=sc_), [rp], [rot])
                                dst = gqT if segn == 'qg' else gkT
                                cc_ = c - SEG[segn][0]
                                S.dma('act', dst[cc_ * 128:(cc_ + 1) * 128, t0:t0 + W], ot[:, :W], reads=[rot])
                            elif segn == 'r':
                                ot, rot = ost.next()
                                S.op('act', lambda e, ot=ot, pp=pp: e.activation(out=ot[:, :W], in_=pp[:, :W], func=AF.Silu), [rp], [rot])
                                cc_ = c - SEG[segn][0]
                                S.dma('act', rsT[cc_ * 128:(cc_ + 1) * 128, t0:t0 + W], ot[:, :W], reads=[rot])
                            elif segn == 'gate':
                                ot, rot = ost.next()
                                cc_ = c - SEG[segn][0]
                                S.op('act', lambda e, ot=ot, pp=pp, cc_=cc_: e.activation(out=ot[:, :W], in_=pp[:, :W], func=AF.Sigmoid,
                                                                                     bias=PV(l, 'b_gate', cc_)), [rp, r_pv[l]], [rot])
                                S.dma('act', gatesT[cc_ * 128:(cc_ + 1) * 128, t0:t0 + W], ot[:, :W], reads=[rot])
                    pp, rp = pr.next()
                    for k in range(KC):
                        S.op('pe', lambda e, pp=pp, k=k: e.matmul(pp[0:32, :W], wlr[:, k, :], h[:, k, :W], start=(k == 0), stop=(k == KC - 1)),
                             [r_h, r_wlr], [rp])
                    ot, rot = ost.next()
                    S.op('act', lambda e, ot=ot, pp=pp: e.copy(out=ot[0:32, :W], in_=pp[0:32, :W]), [rp], [rot])
                    S.dma('act', lrT[:, t0:t0 + W], ot[0:32, :W], reads=[rot])
                    S.reset()
                S.reset()


            if do(2):
              with ExitStack() as ps_:
                inb = [sb("cvin%d" % i, [128, NLAT + 30], stack=ps_) for i in range(2)]; r_inb = [Res(), Res()]
                acc = [sb("cvacc%d" % i, [128, NLAT], stack=ps_) for i in range(2)]
                r_acc = [[Res(), Res()] for _ in range(2)]
                ptmp = sb("cvptmp", [128, NLAT // 3 + 8], stack=ps_); r_ptmp = Res()
                segs = [(0, NLAT)] + ([] if last else [(NLAT, NCTX)])
                it = 0
                cw0 = _pv['caw'][0]
                for c in range(8):
                    for (s0, N) in segs:
                        b = it % 2; it += 1
                        ib, rib = inb[b], r_inb[b]
                        S.op('pool', lambda e: e.memset(ib[:, 0:15], 0.0), [], [rib])
                        S.op('pool', lambda e: e.memset(ib[:, 15 + N:30 + N], 0.0), [], [rib])
                        S.dma('sp', ib[:, 15:15 + N], gluT[c * 128:(c + 1) * 128, s0:s0 + N], writes=[rib])
                        nsp = (2 * N // 3) // 2 * 2
                        for hi, (a0, a1, en) in enumerate([(0, nsp, 'dve'), (nsp, N, 'pool')]):
                            ra = r_acc[b][hi]
                            ac = acc[b]
                            S.op('act', lambda e: e.activation(out=ac[:, a0:a1], in_=ib[:, a0:a1], func=AF.Identity,
                                                               bias=PV(l, 'cab', c), scale=pvt[l][:, cw0 + c * 31:cw0 + c * 31 + 1]),
                                 [rib, r_pv[l]], [ra])
                            for k in range(1, 31):
                                wk = pvt[l][:, cw0 + c * 31 + k:cw0 + c * 31 + k + 1]
                                if en == 'dve':
                                    S.op(en, lambda e, k=k, wk=wk: e.scalar_tensor_tensor(
                                        out=ac[:, a0:a1], in0=ib[:, a0 + k:a1 + k], scalar=wk,
                                        in1=ac[:, a0:a1], op0=ALU.mult, op1=ALU.add), [rib, ra, r_pv[l]], [ra])
                                else:
                                    S.op(en, lambda e, k=k, wk=wk: e.tensor_scalar(
                                        out=ptmp[:, 0:a1 - a0], in0=ib[:, a0 + k:a1 + k], scalar1=wk, scalar2=None, op0=ALU.mult),
                                        [rib, r_pv[l]], [r_ptmp])
                                    S.op(en, lambda e: e.tensor_tensor(out=ac[:, a0:a1], in0=ac[:, a0:a1], in1=ptmp[:, 0:a1 - a0], op=ALU.add),
                                         [ra, r_ptmp], [ra])
                            S.dma('act', convT[c * 128:(c + 1) * 128, s0 + a0:s0 + a1], ac[:, a0:a1], reads=[ra])
                S.reset()

            if do(3):
              with ExitStack() as ps_:
                NKT = T // 128
                kh = [sb("atk%d" % i, [128, T], BF16, stack=ps_) for i in range(2)]; r_kh = [Res(), Res()]
                vh = [sb("atv%d" % i, [128, NKT, 128], BF16, stack=ps_) for i in range(2)]; r_vh = [Res(), Res()]
                qr = Ring([(sb("atq%d" % i, [128, 512], BF16, stack=ps_), Res()) for i in range(2)])
                ptr_ = Ring([(sb("atp%d" % i, [128, 512], BF16, stack=ps_), Res()) for i in range(3)])
                r1 = sb("atr1", [128, 512], stack=ps_); r_r1 = Res()
                r2 = sb("atr2", [128, 512], stack=ps_); r_r2 = Res()
                o1 = sb("ato1", [128, 512], stack=ps_); r_o1 = Res()
                o2 = sb("ato2", [128, 512], stack=ps_); r_o2 = Res()
                sqa = sb("atsq", [128, 512], stack=ps_); r_sqa = Res()
                rstd = sb("atrstd", [128, 512], stack=ps_); r_rstd = Res()
                obr = Ring([(sb("atob%d" % i, [128, 512], BF16, stack=ps_), Res()) for i in range(2)])
                sr = Ring([(psum[0], r_ps[0]), (psum[1], r_ps[1])])
                lat_kt = list(range(NKT))
                ctx_kt = list(range(NLAT // 128, NKT))
                qtiles = [(t, 512, lat_kt) for t in range(0, NLAT, 512)]
                if not last:
                    qtiles.append((NLAT, NCTX, ctx_kt))
                for h in range(8):
                    b = h % 2
                    S.dma('sp', kh[b][:], kT[h * 128:(h + 1) * 128, :], writes=[r_kh[b]])
                    for n0 in range(0, NKT, 8):
                        n1 = min(n0 + 8, NKT)
                        S.dma('sp', vh[b][:, n0:n1, :], vtm.rearrange("(n p) c -> p n c", p=128)[:, n0:n1, h * 128:(h + 1) * 128], writes=[r_vh[b]])
                    for (t0, W, kts) in qtiles:
                        qt_, rq = qr.next()
                        S.dma('sp', qt_[:, :W], qT[h * 128:(h + 1) * 128, t0:t0 + W], writes=[rq])
                        for comp in range(2):
                            pO, rO = (psum[2], r_ps[2]) if comp == 0 else (psum[4], r_ps[4])
                            pL, rL = (psum[3], r_ps[3]) if comp == 0 else (psum[5], r_ps[5])
                            lo = comp * 64

                            def emit_s(kt):
                                pS, rS = sr.next()
                                S.op('pe', lambda e: e.matmul(pS[:, :W], kh[b][lo:lo + 64, kt * 128:(kt + 1) * 128], qt_[lo:lo + 64, :W],
                                                              start=True, stop=True), [r_kh[b], rq], [rS])
                                return pS, rS
                            cur = emit_s(kts[0])
                            for idx, kt in enumerate(kts):
                                nxt = emit_s(kts[idx + 1]) if idx + 1 < len(kts) else None
                                pS, rS = cur
                                pt_, rpt = ptr_.next()
                                S.op('act', lambda e, pS=pS, pt_=pt_: e.activation(out=pt_[:, :W], in_=pS[:, :W], func=AF.Exp), [rS], [rpt])
                                S.op('pe', lambda e, pt_=pt_, kt=kt, idx=idx: e.matmul(pO[:, :W], vh[b][:, kt, :], pt_[:, :W], start=(idx == 0),
                                                                                     stop=(idx == len(kts) - 1)), [r_vh[b], rpt], [rO])
                                S.op('pe', lambda e, pt_=pt_, idx=idx: e.matmul(pL[:, :W], ones_bf, pt_[:, :W], start=(idx == 0),
                                                                              stop=(idx == len(kts) - 1)), [r_cbf, rpt], [rL])
                                cur = nxt
                        S.op('dve', lambda e: e.reciprocal(out=r1[:, :W], in_=psum[3][:, :W]), [r_ps[3]], [r_r1])
                        S.op('dve', lambda e: e.reciprocal(out=r2[:, :W], in_=psum[5][:, :W]), [r_ps[5]], [r_r2])
                        S.op('dve', lambda e: e.tensor_tensor(out=o1[:, :W], in0=psum[2][:, :W], in1=r1[:, :W], op=ALU.mult), [r_ps[2], r_r1], [r_o1])
                        S.op('dve', lambda e: e.scalar_tensor_tensor(out=o2[:, :W], in0=psum[4][:, :W], scalar=lamt[l][:, 0:1], in1=r2[:, :W],
                                                                     op0=ALU.mult, op1=ALU.mult), [r_ps[4], r_r2, r_lam[l]], [r_o2])
                        S.op('pool', lambda e: e.tensor_tensor(out=o1[:, :W], in0=o1[:, :W], in1=o2[:, :W], op=ALU.subtract), [r_o1, r_o2], [r_o1])
                        S.op('act', lambda e: e.activation(out=sqa[:, :W], in_=o1[:, :W], func=AF.Square), [r_o1], [r_sqa])
                        S.op('pe', lambda e: e.matmul(psum[6][:, :W], C('ones'), sqa[:, :W], start=True, stop=True), [r_sqa, r_consts], [r_ps[6]])
                        rms_rstd(psum[6], r_ps[6], 128, rstd, r_rstd, W)
                        S.op('dve', lambda e: e.tensor_tensor(out=o1[:, :W], in0=o1[:, :W], in1=rstd[:, :W], op=ALU.mult), [r_o1, r_rstd], [r_o1])
                        ob, rob = obr.next()
                        S.op('act', lambda e, ob=ob: e.activation(out=ob[:, :W], in_=o1[:, :W], func=AF.Copy, scale=lamt[l][:, 1:2]), [r_o1, r_lam[l]], [rob])
                        S.dma('act', attnT[h * 128:(h + 1) * 128, t0:t0 + W], ob[:, :W], reads=[rob])
                    S.reset()
                S.reset()

            if do(4):
              with ExitStack() as ps_:
                w2t = sb("glw2", [32, 2, 512], stack=ps_); r_w2t = Res()
                S.dma('sp', w2t[:], w2p[l], writes=[r_w2t])
                Sst = [sb("glS%d" % h, [128, 256], stack=ps_) for h in range(4)]; r_S = [Res() for _ in range(4)]
                Sbf = [sb("glSb%d" % h, [128, 256], BF16, stack=ps_) for h in range(4)]; r_Sbf = [Res() for _ in range(4)]
                ldr = Ring([dict(gq=sb("glq%d" % i, [128, 4, 128], stack=ps_), gk=sb("glk%d" % i, [128, 4, 128], stack=ps_),
                                 gv=sb("glv%d" % i, [128, 1024], BF16, stack=ps_), lr=sb("gllr%d" % i, [32, 128], stack=ps_),
                                 of=sb("glof%d" % i, [128, 8, 128], stack=ps_), rs=sb("glrs%d" % i, [128, 8, 128], stack=ps_),
                                 r=Res()) for i in range(2)])
                zb = sb("glzb", [128, 512], stack=ps_); r_zb = Res()
                spt = sb("glsp", [128, 512], stack=ps_); r_sp = Res()
                hr = Ring([dict(Ep=sb("glEp%d" % i, [128, 128], stack=ps_), Em=sb("glEm%d" % i, [128, 128], stack=ps_),
                                qd=sb("glqd%d" % i, [128, 128], BF16, stack=ps_), kd=sb("glkd%d" % i, [128, 128], BF16, stack=ps_),
                                kdec=sb("glkdec%d" % i, [128, 128], BF16, stack=ps_), kdT=sb("glkdT%d" % i, [128, 128], BF16, stack=ps_),
                                atm=sb("glatm%d" % i, [128, 128], BF16, stack=ps_), ot=sb("glot%d" % i, [128, 2, 128], stack=ps_),
                                sq=sb("glsq%d" % i, [128, 2, 128], stack=ps_), rstd=sb("glrstd%d" % i, [128, 128], stack=ps_),
                                y=sb("gly%d" % i, [128, 2, 128], stack=ps_),
                                rE=Res(), rq=Res(), rk=Res(), rkdec=Res(), rkdT=Res(), ratm=Res(), rot=Res(), rsq=Res(), rrstd=Res(), ry=Res())
                           for i in range(2)])
                osum = sb("glosum", [128, 8, 128], stack=ps_); r_osum = Res()
                ybr = Ring([(sb("glyb%d" % i, [128, 8, 128], BF16, stack=ps_), Res()) for i in range(2)])
                por = Ring([(psum[3], r_ps[3]), (psum[6], r_ps[6])])
                lat_ch = [(c * 128, False) for c in range(NLAT // 128)]
                ctx_ch = [(NLAT + c * 128, True) for c in range(NCTX // 128)]
                bb0 = _pv['bb'][0]
                for d in range(2):
                    for h in range(4):
                        S.op('pool', lambda e, h=h: e.memset(Sst[h][:], 0.0), [], [r_S[h]])
                        S.op('pool', lambda e, h=h: e.memset(Sbf[h][:], 0.0), [], [r_Sbf[h]])
                    order = (ctx_ch + lat_ch) if d == 0 else (ctx_ch[::-1] + lat_ch[::-1])
                    tri = C('tri_f' if d == 0 else 'tri_b')
                    msk = C('mask_f' if d == 0 else 'mask_b')
                    lc = 127 if d == 0 else 0
                    for oi, (t0, is_ctx) in enumerate(order):
                        if oi and oi % 8 == 0:
                            S.reset()
                        need_out = not (last and is_ctx)
                        L_ = ldr.next(); rl = L_['r']
                        S.dma('sp', L_['gq'][:], gqT.rearrange("(h p) t -> p h t", p=128)[:, :, t0:t0 + 128], writes=[rl])
                        S.dma('sp', L_['gk'][:], gkT.rearrange("(h p) t -> p h t", p=128)[:, :, t0:t0 + 128], writes=[rl])
                        S.dma('sp', L_['gv'][:], gvtm[t0:t0 + 128, :], writes=[rl])
                        S.dma('sp', L_['lr'][:], lrT[:, t0:t0 + 128], writes=[rl])
                        if d == 1 and need_out:
                            S.dma('sp', L_['of'][:], ofT.rearrange("(c p) t -> p c t", p=128)[:, :, t0:t0 + 128], writes=[rl])
                            S.dma('sp', L_['rs'][:], rsT.rearrange("(c p) t -> p c t", p=128)[:, :, t0:t0 + 128], writes=[rl])
                        S.op('pe', lambda e: e.matmul(psum[0][:, :512], L_['lr'][:, :], w2t[:, d, :], start=True, stop=True), [rl, r_w2t], [r_ps[0]])
                        S.op('dve', lambda e: e.tensor_tensor(out=zb[:], in0=psum[0][:, :512], in1=pvt[l][:, bb0 + d * 512:bb0 + (d + 1) * 512], op=ALU.add),
                             [r_ps[0], r_pv[l]], [r_zb])
                        S.op('act', lambda e: e.activation(out=zb[:], in_=zb[:], func=AF.Exp, scale=-1.0), [r_zb], [r_zb])
                        S.op('act', lambda e: e.activation(out=spt[:], in_=zb[:], func=AF.Ln, bias=one_ap, scale=1.0), [r_zb, r_eps], [r_sp])
                        if d == 1 and need_out:
                            yb, ryb = ybr.next()
                        for h in range(4):
                            H = hr.next()
                            S.op('pe', lambda e: e.matmul(psum[1][:, :128], spt[:, h * 128:(h + 1) * 128], tri, start=True, stop=True), [r_sp, r_consts], [r_ps[1]])
                            S.op('act', lambda e: e.activation(out=H['Ep'][:], in_=psum[1][:, :128], func=AF.Exp), [r_ps[1]], [H['rE']])
                            S.op('act', lambda e: e.activation(out=H['Em'][:], in_=psum[1][:, :128], func=AF.Exp, scale=-1.0), [r_ps[1]], [H['rE']])
                            S.op('dve', lambda e: e.tensor_tensor(out=H['qd'][:], in0=L_['gq'][:, h, :], in1=H['Ep'][:], op=ALU.mult), [rl, H['rE']], [H['rq']])
                            S.op('dve', lambda e: e.tensor_tensor(out=H['kd'][:], in0=L_['gk'][:, h, :], in1=H['Em'][:], op=ALU.mult), [rl, H['rE']], [H['rk']])
                            S.op('dve', lambda e: e.scalar_tensor_tensor(out=H['kdec'][:], in0=L_['gk'][:, h, :], scalar=H['Ep'][:, lc:lc + 1], in1=H['Em'][:],
                                                                         op0=ALU.mult, op1=ALU.mult), [rl, H['rE']], [H['rkdec']])
                            if need_out:
                                S.op('pe', lambda e: e.matmul(psum[2][:, :128], H['kd'][:], H['qd'][:], start=True, stop=True), [H['rk'], H['rq']], [r_ps[2]])
                                S.op('dve', lambda e: e.tensor_tensor(out=H['atm'][:], in0=psum[2][:, :128], in1=msk, op=ALU.mult), [r_ps[2], r_consts], [H['ratm']])
                                po, rpo = por.next()
                                for vc in range(2):
                                    S.op('pe', lambda e, vc=vc: e.matmul(po[:, vc * 128:(vc + 1) * 128], L_['gv'][:, h * 256 + vc * 128:h * 256 + (vc + 1) * 128],
                                                                         H['atm'][:], start=True, stop=False), [rl, H['ratm']], [rpo])
                                    S.op('pe', lambda e, vc=vc: e.matmul(po[:, vc * 128:(vc + 1) * 128], Sbf[h][:, vc * 128:(vc + 1) * 128],
                                                                         H['qd'][:], start=False, stop=True), [r_Sbf[h], H['rq']], [rpo])
                            S.op('pe', lambda e: e.transpose(psb[:, :128], H['kdec'][:], ident_bf), [H['rkdec'], r_cbf], [r_psb])
                            S.op('act', lambda e: e.copy(out=H['kdT'][:], in_=psb[:, :128]), [r_psb], [H['rkdT']])
                            S.op('pe', lambda e: e.matmul(psum[4][:, :256], H['kdT'][:], L_['gv'][:, h * 256:(h + 1) * 256], start=True, stop=True),
                                 [H['rkdT'], rl], [r_ps[4]])
                            S.op('dve', lambda e: e.scalar_tensor_tensor(out=Sst[h][:], in0=Sst[h][:], scalar=H['Ep'][:, lc:lc + 1], in1=psum[4][:, :256],
                                                                         op0=ALU.mult, op1=ALU.add), [r_S[h], H['rE'], r_ps[4]], [r_S[h]])
                            S.op('pool', lambda e: e.tensor_copy(out=Sbf[h][:], in_=Sst[h][:]), [r_S[h]], [r_Sbf[h]])
                            if not need_out:
                                continue
                            if d == 0:
                                S.op('act', lambda e: e.copy(out=H['ot'][:], in_=po[:, :256].rearrange("p (c n) -> p c n", c=2)), [rpo], [H['rot']])
                                S.dma('act', ofT.rearrange("(c p) t -> p c t", p=128)[:, 2 * h:2 * h + 2, t0:t0 + 128], H['ot'][:], reads=[H['rot']])
                            else:
                                S.op('dve', lambda e: e.tensor_tensor(out=osum[:, 2 * h:2 * h + 2, :], in0=L_['of'][:, 2 * h:2 * h + 2, :],
                                                                      in1=po[:, :256].rearrange("p (c n) -> p c n", c=2), op=ALU.add), [rl, rpo], [r_osum])
                                S.op('act', lambda e: e.activation(out=H['sq'][:], in_=osum[:, 2 * h:2 * h + 2, :], func=AF.Square), [r_osum], [H['rsq']])
                                for vc in range(2):
                                    S.op('pe', lambda e, vc=vc: e.matmul(psum[5][:, :128], C('ones'), H['sq'][:, vc, :], start=(vc == 0), stop=(vc == 1)),
                                         [H['rsq'], r_consts], [r_ps[5]])
                                rms_rstd(psum[5], r_ps[5], 256, H['rstd'], H['rrstd'], 128)
                                for vc in range(2):
                                    S.op('dve', lambda e, vc=vc: e.scalar_tensor_tensor(out=H['y'][:, vc, :], in0=osum[:, 2 * h + vc, :], scalar=PV(l, 'gn', vc),
                                                                                        in1=H['rstd'][:], op0=ALU.mult, op1=ALU.mult),
                                         [r_osum, r_pv[l], H['rrstd']], [H['ry']])
                                S.op('pool', lambda e: e.tensor_tensor(out=yb[:, 2 * h:2 * h + 2, :], in0=H['y'][:], in1=L_['rs'][:, 2 * h:2 * h + 2, :], op=ALU.mult),
                                     [H['ry'], rl], [ryb])
                        if d == 1 and need_out:
                            S.dma('act', glaT.rearrange("(c p) t -> p c t", p=128)[:, :, t0:t0 + 128], yb[:], reads=[ryb])
                    S.reset()

            if do(5):
              with ExitStack() as ps_:
                cv = sb("mgcv", [128, 8, 512], stack=ps_); r_cv = Res()
                at = sb("mgat", [128, 8, 512], BF16, stack=ps_); r_at = Res()
                gl = sb("mggl", [128, 8, 512], BF16, stack=ps_); r_gl = Res()
                ha = sb("mgha", [128, 8, 512], BF16, stack=ps_); r_ha = Res()
                xt = sb("mgx", [128, KC, 512], stack=ps_); r_xt = Res()
                mt = sb("mgm", [128, KC, 512], BF16, stack=ps_); r_mt = Res()
                gtr = Ring([(sb("mggt%d" % i, [128, 3, 512], stack=ps_), Res()) for i in range(2)])
                sqr = Ring([(sb("mgsq%d" % i, [128, 512], stack=ps_), Res()) for i in range(2)])
                mean = sb("mgmean", [128, 512], stack=ps_); r_mean = Res()
                msq = sb("mgmsq", [128, 512], stack=ps_); r_msq = Res()
                rstd = sb("mgrstd", [128, 512], stack=ps_); r_rstd = Res()
                tmr = Ring([(sb("mgtm%d" % i, [128, 512], stack=ps_), Res()) for i in range(4)])
                wabc = Ring([(sb("mgwabc%d" % i, [128, 3, 8, 128], BF16, stack=ps_), Res()) for i in range(2)])
                wor = Ring([(sb("mgwo%d" % i, [128, KC, 128], BF16, stack=ps_), Res()) for i in range(3)])
                stg_alloc(ps_)
                ost = Ring([(sb("mgo%d" % i, [128, 512], stack=ps_), Res()) for i in range(3)])
                pr = Ring([(psum[i], r_ps[i]) for i in range(2, 7)])
                for (t0, W, j) in tok_tiles(last):
                    if last and j == 1:
                        continue
                    fm = lambda ap: ap.rearrange("(k p) t -> p k t", p=128)[:, :, t0:t0 + W]
                    S.dma('sp', cv[:, :, :W], fm(convT), writes=[r_cv])
                    S.dma('sp', at[:, :, :W], fm(attnT), writes=[r_at])
                    S.dma('sp', gl[:, :, :W], fm(glaT), writes=[r_gl])
                    S.dma('sp', xt[:, :, :W], fm(xin), writes=[r_xt])
                    s1, rs1 = psum[0], r_ps[0]
                    s2, rs2_ = psum[1], r_ps[1]
                    for k in range(8):
                        sqt, rsq = sqr.next()
                        S.op('pe', lambda e, k=k: e.matmul(s1[:, :W], C('ones'), cv[:, k, :W], start=(k == 0), stop=(k == 7)), [r_cv, r_consts], [rs1])
                        S.op('act', lambda e, k=k, sqt=sqt: e.activation(out=sqt[:, :W], in_=cv[:, k, :W], func=AF.Square), [r_cv], [rsq])
                        S.op('pe', lambda e, k=k, sqt=sqt: e.matmul(s2[:, :W], C('ones'), sqt[:, :W], start=(k == 0), stop=(k == 7)), [rsq, r_consts], [rs2_])
                    S.op('act', lambda e: e.activation(out=mean[:, :W], in_=s1[:, :W], func=AF.Copy, scale=1.0 / DA), [rs1], [r_mean])
                    S.op('dve', lambda e: e.tensor_tensor(out=msq[:, :W], in0=mean[:, :W], in1=mean[:, :W], op=ALU.mult), [r_mean], [r_msq])
                    S.op('dve', lambda e: e.scalar_tensor_tensor(out=msq[:, :W], in0=s2[:, :W], scalar=1.0 / DA, in1=msq[:, :W],
                                                                 op0=ALU.mult, op1=ALU.subtract), [rs2_, r_msq], [r_msq])
                    S.op('act', lambda e: e.activation(out=rstd[:, :W], in_=msq[:, :W], func=AF.Sqrt, bias=eps_ap, scale=1.0), [r_msq, r_eps], [r_rstd])
                    S.op('dve', lambda e: e.reciprocal(out=rstd[:, :W], in_=rstd[:, :W]), [r_rstd], [r_rstd])
                    for k in range(8):
                        tt, rtt = tmr.next()
                        S.op('dve', lambda e, k=k, tt=tt: e.tensor_tensor(out=tt[:, :W], in0=cv[:, k, :W], in1=mean[:, :W], op=ALU.subtract), [r_cv, r_mean], [rtt])
                        S.op('pool', lambda e, tt=tt: e.tensor_tensor(out=tt[:, :W], in0=tt[:, :W], in1=rstd[:, :W], op=ALU.mult), [rtt, r_rstd], [rtt])
                        S.op('act', lambda e, k=k, tt=tt: e.activation(out=ha[:, k, :W], in_=tt[:, :W], func=AF.Silu,
                                                                     bias=PV(l, 'lab', k), scale=PV(l, 'lag', k)), [rtt, r_pv[l]], [r_ha])
                    for oc in range(16):
                        wb, rw = wabc.next()
                        wload(wb[:].rearrange("p b k n -> p b (k n)"), w_abc[l, :, oc].rearrange("b p k n -> p b (k n)"), rw, 3072)
                        gt_, rgt = gtr.next()
                        S.dma('sp', gt_[:, :, :W], gatesT.rearrange("(b c p) t -> p b c t", b=3, c=16)[:, :, oc, t0:t0 + W], writes=[rgt])
                        pp3 = []
                        for br, (src, rsrc) in enumerate(((ha, r_ha), (at, r_at), (gl, r_gl))):
                            pp, rp = pr.next()
                            for k in range(8):
                                S.op('pe', lambda e, pp=pp, br=br, k=k, src=src, wb=wb: e.matmul(
                                    pp[:, :W], wb[:, br, k, :], src[:, k, :W], start=(k == 0), stop=(k == 7)), [rw, rsrc], [rp])
                            pp3.append((pp, rp))
                        ta, rta = tmr.next()
                        tb, rtb = tmr.next()
                        S.op('dve', lambda e, ta=ta, gt_=gt_: e.tensor_tensor(out=ta[:, :W], in0=pp3[0][0][:, :W], in1=gt_[:, 0, :W], op=ALU.mult), [pp3[0][1], rgt], [rta])
                        S.op('dve', lambda e, tb=tb, gt_=gt_: e.tensor_tensor(out=tb[:, :W], in0=pp3[1][0][:, :W], in1=gt_[:, 1, :W], op=ALU.mult), [pp3[1][1], rgt], [rtb])
                        S.op('pool', lambda e, ta=ta, tb=tb: e.tensor_tensor(out=ta[:, :W], in0=ta[:, :W], in1=tb[:, :W], op=ALU.add), [rta, rtb], [rta])
                        tc_, rtc = tmr.next()
                        S.op('dve', lambda e, tc_=tc_, gt_=gt_: e.tensor_tensor(out=tc_[:, :W], in0=pp3[2][0][:, :W], in1=gt_[:, 2, :W], op=ALU.mult), [pp3[2][1], rgt], [rtc])
                        S.op('pool', lambda e, ta=ta, tc_=tc_, oc=oc: e.tensor_tensor(out=mt[:, oc, :W], in0=ta[:, :W], in1=tc_[:, :W], op=ALU.add), [rta, rtc], [r_mt])
                    for oc in range(16):
                        wb, rw = wor.next()
                        wload(wb[:].rearrange("p k n -> p (k n)"), w_o[l, oc].rearrange("p k n -> p (k n)"), rw, 2048)
                        pp, rp = pr.next()
                        for k in range(KC):
                            S.op('pe', lambda e, pp=pp, k=k, wb=wb: e.matmul(pp[:, :W], wb[:, k, :], mt[:, k, :W], start=(k == 0), stop=(k == KC - 1)), [rw, r_mt], [rp])
                        ot, rot = ost.next()
                        S.op('dve', lambda e, ot=ot, pp=pp, oc=oc: e.scalar_tensor_tensor(
                            out=ot[:, :W], in0=pp[:, :W], scalar=MOD(l, 'gt1', oc, j), in1=xt[:, oc, :W], op0=ALU.mult, op1=ALU.add),
                            [rp, r_mods[l], r_xt], [rot])
                        S.dma('act', xa[oc * 128:(oc + 1) * 128, t0:t0 + W], ot[:, :W], reads=[rot])
                    S.reset()
                S.reset()

            if do(6):
              with ExitStack() as ps_:
                xt = sb("ffx", [128, KC, 512], stack=ps_); r_xt = Res()
                h2 = sb("ffh", [128, KC, 512], BF16, stack=ps_); r_h2 = Res()
                actb = sb("ffact", [128, 44, 512], BF16, stack=ps_); r_actb = Res()
                sqr = Ring([(sb("ffsq%d" % i, [128, 512], stack=ps_), Res()) for i in range(2)])
                rstd = sb("ffrstd", [128, 512], stack=ps_); r_rstd = Res()
                wur = Ring([(sb("ffwu%d" % i, [128, 2, KC, 128], BF16, stack=ps_), Res()) for i in range(2)])
                stg_alloc(ps_)
                wdr = Ring([(sb("ffwd%d" % i, [128, 44, 128], BF16, stack=ps_), Res()) for i in range(2)])
                tvr = Ring([(sb("fftv%d" % i, [128, 512], stack=ps_), Res()) for i in range(2)])
                tgr = Ring([(sb("fftg%d" % i, [128, 512], stack=ps_), Res()) for i in range(2)])
                ost = Ring([(sb("ffo%d" % i, [128, 512], stack=ps_), Res()) for i in range(3)])
                pr = Ring([(psum[i], r_ps[i]) for i in range(1, 7)])
                cf0 = _pv['cfw'][0]
                segs = [(0, NLAT, 0)] + ([] if last else [(NLAT, NCTX, 1)])
                for (s0, N, j) in segs:
                    for (a, b_) in _ffn_tiles(N):
                        Wt = b_ - a
                        Wc = Wt + 2
                        lo, hi = max(a - 1, 0), min(b_ + 1, N)
                        off = lo - (a - 1)
                        if a == 0:
                            S.op('pool', lambda e: e.memset(xt[:, :, 0:1], 0.0), [], [r_xt])
                        if b_ == N:
                            S.op('pool', lambda e: e.memset(xt[:, :, Wc - 1:Wc], 0.0), [], [r_xt])
                        S.dma('sp', xt[:, :, off:off + hi - lo], xa.rearrange("(k p) t -> p k t", p=128)[:, :, s0 + lo:s0 + hi], writes=[r_xt])
                        ssp, r_ssp = psum[0], r_ps[0]
                        for k in range(KC):
                            sqt, rsq = sqr.next()
                            S.op('act', lambda e, k=k, sqt=sqt: e.activation(out=sqt[:, :Wc], in_=xt[:, k, :Wc], func=AF.Square), [r_xt], [rsq])
                            S.op('pe', lambda e, k=k, sqt=sqt: e.matmul(ssp[:, :Wc], C('ones'), sqt[:, :Wc], start=(k == 0), stop=(k == KC - 1)), [rsq, r_consts], [r_ssp])
                        rms_rstd(ssp, r_ssp, D, rstd, r_rstd, Wc)
                        for k in range(KC):
                            sqt, rsq = sqr.next()
                            S.op('dve', lambda e, k=k, sqt=sqt: e.tensor_tensor(out=sqt[:, :Wc], in0=xt[:, k, :Wc], in1=rstd[:, :Wc], op=ALU.mult), [r_xt, r_rstd], [rsq])
                            S.op('act', lambda e, k=k, sqt=sqt: e.activation(out=h2[:, k, :Wc], in_=sqt[:, :Wc], func=AF.Identity,
                                                                           bias=MOD(l, 'sh2', k, j), scale=m1p[l][:, 16 + k, j:j + 1]), [rsq, r_mods[l]], [r_h2])
                        if a == 0:
                            S.op('pool', lambda e: e.memset(h2[:, :, 0:1], 0.0), [], [r_h2])
                        if b_ == N:
                            S.op('pool', lambda e: e.memset(h2[:, :, Wc - 1:Wc], 0.0), [], [r_h2])
                        for c in range(44):
                            wb, rw = wur.next()
                            wload(wb[:, 0].rearrange("p k n -> p (k n)"), w_up[l, c].rearrange("p k n -> p (k n)"), rw, 2048)
                            wload(wb[:, 1].rearrange("p k n -> p (k n)"), w_up[l, 44 + c].rearrange("p k n -> p (k n)"), rw, 2048)
                            pv_, rpv = pr.next()
                            pg_, rpg = pr.next()
                            for hh, (pp, rp) in enumerate(((pv_, rpv), (pg_, rpg))):
                                for k in range(KC):
                                    S.op('pe', lambda e, pp=pp, hh=hh, k=k, wb=wb: e.matmul(pp[:, :Wc], wb[:, hh, k, :], h2[:, k, :Wc],
                                                                                           start=(k == 0), stop=(k == KC - 1)), [rw, r_h2], [rp])
                            tv, rtv = tvr.next()
                            tg, rtg = tgr.next()
                            for (pp, rp, tt, rtt, ch) in ((pv_, rpv, tv, rtv, c), (pg_, rpg, tg, rtg, 44 + c)):
                                wcol = lambda kk, ch=ch: pvt[l][:, cf0 + ch * 3 + kk:cf0 + ch * 3 + kk + 1]
                                S.op('act', lambda e, pp=pp, tt=tt, ch=ch, wcol=wcol: e.activation(out=tt[:, :Wt], in_=pp[:, 1:Wt + 1], func=AF.Identity,
                                                                                                 bias=PV(l, 'cfb', ch), scale=wcol(1)), [rp, r_pv[l]], [rtt])
                                S.op('dve', lambda e, pp=pp, tt=tt, wcol=wcol: e.scalar_tensor_tensor(out=tt[:, :Wt], in0=pp[:, 0:Wt], scalar=wcol(0), in1=tt[:, :Wt],
                                                                                                    op0=ALU.mult, op1=ALU.add), [rp, rtt, r_pv[l]], [rtt])
                                S.op('dve', lambda e, pp=pp, tt=tt, wcol=wcol: e.scalar_tensor_tensor(out=tt[:, :Wt], in0=pp[:, 2:Wt + 2], scalar=wcol(2), in1=tt[:, :Wt],
                                                                                                    op0=ALU.mult, op1=ALU.add), [rp, rtt, r_pv[l]], [rtt])
                            S.op('act', lambda e, tg=tg: e.activation(out=tg[:, :Wt], in_=tg[:, :Wt], func=AF.Silu), [rtg], [rtg])
                            S.op('pool', lambda e, tg=tg, tv=tv, c=c: e.tensor_tensor(out=actb[:, c, :Wt], in0=tg[:, :Wt], in1=tv[:, :Wt], op=ALU.mult), [rtg, rtv], [r_actb])
                        for oc in range(16):
                            wb, rw = wdr.next()
                            for hh in range(2):
                                wload(wb[:, 22 * hh:22 * hh + 22].rearrange("p k n -> p (k n)"),
                                      w_down[l, oc, :, 22 * hh:22 * hh + 22].rearrange("p k n -> p (k n)"), rw, 22 * 128)
                            pp, rp = pr.next()
                            for c in range(44):
                                S.op('pe', lambda e, pp=pp, c=c, wb=wb: e.matmul(pp[:, :Wt], wb[:, c, :], actb[:, c, :Wt], start=(c == 0), stop=(c == 43)), [rw, r_actb], [rp])
                            ot, rot = ost.next()
                            S.op('dve', lambda e, ot=ot, pp=pp, oc=oc: e.scalar_tensor_tensor(
                                out=ot[:, :Wt], in0=pp[:, :Wt], scalar=MOD(l, 'gt2', oc, j), in1=xt[:, oc, 1:1 + Wt], op0=ALU.mult, op1=ALU.add),
                                [rp, r_mods[l], r_xt], [rot])
                            dst = yT if last else xb
                            S.dma('act', dst[oc * 128:(oc + 1) * 128, s0 + a:s0 + b_], ot[:, :Wt], reads=[rot])
                        S.reset()
                S.reset()

        S.barrier()
    print("[build] instructions=%d waits=%d" % (S.n_ins, S.n_wait))
    return nc


def _blk(w, kc):
    K, N = w.shape
    return np.ascontiguousarray(w.reshape(kc, 128, N // 128, 128).transpose(2, 1, 0, 3))


def _pp(v):
    return np.ascontiguousarray(v.reshape(-1, 128).T)


def host_consts():
    c = np.zeros((128, NCONST), np.float32)
    i = np.arange(128)
    c[:, _cv['ident']:_cv['ident'] + 128] = np.eye(128, dtype=np.float32)
    c[:, _cv['ones']:_cv['ones'] + 128] = 1.0
    bo = (i[:, None] // 64 == i[None, :] // 64).astype(np.float32)
    c[:, _cv['bones64']:_cv['bones64'] + 128] = bo
    R = np.zeros((128, 128), np.float32)
    for d in range(128):
        dd = d % 64
        base = d - dd
        if dd < 32:
            R[d, base + dd + 32] = -1.0
        else:
            R[d, base + dd - 32] = 1.0
    c[:, _cv['rotT']:_cv['rotT'] + 128] = R.T
    jj, ii = i[:, None], i[None, :]
    c[:, _cv['tri_f']:_cv['tri_f'] + 128] = (jj <= ii) * (-1.0 / 16.0)
    c[:, _cv['tri_b']:_cv['tri_b'] + 128] = (jj >= ii) * (-1.0 / 16.0)
    c[:, _cv['mask_f']:_cv['mask_f'] + 128] = (jj <= ii)
    c[:, _cv['mask_b']:_cv['mask_b'] + 128] = (jj >= ii)
    return c


def host_rope(NLAT, NCTX, grid_w=64):
    rows = NLAT // grid_w
    row = np.repeat(np.arange(rows, dtype=np.float32), grid_w)
    col = np.tile(np.arange(grid_w, dtype=np.float32), rows)
    inv = (10000.0 ** (-np.arange(16, dtype=np.float32) / 16)).astype(np.float32)
    ang = np.concatenate([row[:, None] * inv, col[:, None] * inv], axis=-1)
    cos = np.cos(ang).astype(np.float32); sin = np.sin(ang).astype(np.float32)
    T = NLAT + NCTX
    cT = np.ones((128, T), np.float32); sT = np.zeros((128, T), np.float32)
    for p in range(128):
        f = p % 32
        cT[p, :NLAT] = cos[:, f]
        sT[p, :NLAT] = sin[:, f]
    return cT, sT


def host_weights(inp, DEPTH=2):
    out = {}
    offs = np.cumsum([0, 2048, 1024, 1024, 1024, 512, 512, 1024, 1024, 32, 6144])
    a0, q0, k0, v0, gq0, gk0, gv0, r0, lr0, gt0 = offs[:10]
    perm = []
    for c in range(8):
        perm += list(range(a0 + c * 128, a0 + (c + 1) * 128))
        perm += list(range(a0 + 1024 + c * 128, a0 + 1024 + (c + 1) * 128))
    perm += list(range(q0, r0 + 1024))
    perm += list(range(gt0, gt0 + 6144))
    perm = np.array(perm)
    assert perm.size == NCH_IN * 128
    w_in = inp['w_in']
    out['w_in'] = np.stack([_blk(w_in[l][:, perm], KC) for l in range(DEPTH)])
    out['w_lr'] = np.stack([np.ascontiguousarray(w_in[l][:, lr0:lr0 + 32].reshape(KC, 128, 32).transpose(1, 0, 2)) for l in range(DEPTH)])
    out['w_ada'] = np.stack([_blk(inp['w_ada'][l], KC) for l in range(DEPTH)])
    w2p = np.zeros((DEPTH, 32, 2, 512), np.float32)
    for l in range(DEPTH):
        w2p[l, 0:16, 0, :] = inp['w_alpha2'][l, 0]
        w2p[l, 16:32, 1, :] = inp['w_alpha2'][l, 1]
    out['w2p'] = w2p
    out['w_abc'] = np.stack([np.stack([_blk(inp[n][l], 8) for n in ('w_a_out', 'w_b_out', 'w_c_out')]) for l in range(DEPTH)])
    out['w_o'] = np.stack([_blk(inp['w_o'][l], KC) for l in range(DEPTH)])
    out['w_up'] = np.stack([_blk(inp['w_up'][l], KC) for l in range(DEPTH)])
    out['w_down'] = np.stack([_blk(inp['w_down'][l], 44) for l in range(DEPTH)])
    pv = np.zeros((DEPTH, 128, NPV), np.float32)

    def put(l, name, arr):
        o, w = _pv[name]
        pv[l, :, o:o + w] = arr.reshape(128, w)
    for l in range(DEPTH):
        put(l, 'g1', _pp(inp['g_norm1'][l])); put(l, 'g2', _pp(inp['g_norm2'][l]))
        put(l, 'b_ada', _pp(inp['b_ada'][l])); put(l, 'b_gate', _pp(inp['b_gate'][l].reshape(-1)))
        put(l, 'caw', inp['conv_a_w'][l].reshape(31, 8, 128).transpose(2, 1, 0))
        put(l, 'cab', _pp(inp['conv_a_b'][l])); put(l, 'lag', _pp(inp['ln_a_g'][l])); put(l, 'lab', _pp(inp['ln_a_b'][l]))
        put(l, 'qn', np.tile(inp['qn_g'][l], 2)); put(l, 'kn', np.tile(inp['kn_g'][l], 2))
        put(l, 'subln', inp['subln_g'][l]); put(l, 'gn', _pp(inp['gn_c_g'][l]))
        put(l, 'cfw', inp['conv_f_w'][l].reshape(3, 88, 128).transpose(2, 1, 0))
        put(l, 'cfb', _pp(inp['conv_f_b'][l]))
        put(l, 'bb', np.broadcast_to(inp['b_alpha'][l].reshape(1, 1024), (128, 1024)))
        put(l, 'lam', np.broadcast_to(np.concatenate([inp['lam_q1'][l], inp['lam_k1'][l], inp['lam_q2'][l], inp['lam_k2'][l]])[None, :], (128, 256)))
    out['pv'] = pv
    return out


def host_core_inputs(inp, b, shared):
    x0 = np.ascontiguousarray(np.concatenate([inp['x'][b].T, inp['ctx'][b].T], axis=1))
    cc = np.stack([_pp(inp['c'][b]), _pp(inp['c_ctx'])], axis=-1)
    m = dict(shared)
    m['x0'] = x0
    m['cc'] = np.ascontiguousarray(cc)
    return m


_NC_CACHE = {}


def kernel(**inputs):
    inp = {k: np.asarray(v) for k, v in inputs.items()}
    B, NLAT, _ = inp['x'].shape
    NCTX = inp['ctx'].shape[1]
    shared = host_weights(inp)
    shared['consts'] = host_consts()
    shared['cosT'], shared['sinT'] = host_rope(NLAT, NCTX)
    key = (NLAT, NCTX)
    if key not in _NC_CACHE:
        _NC_CACHE[key] = build(NLAT, NCTX)
    nc = _NC_CACHE[key]
    in_maps = [host_core_inputs(inp, i, shared) for i in range(B)]
    res = run_bass_kernel_spmd(nc, in_maps, core_ids=list(range(B)))
    out = np.stack([res.results[b]["yT"].T for b in range(B)])
    return np.ascontiguousarray(out.astype(np.float32))
```
